# Optimizing a Trainium2 kernel written in Bass

```python
import math
import jax, jax.numpy as jnp
from jax import lax
import numpy as np


D_MODEL = 1024
BATCH = 2
SEQ = 16384
DEPTH = 2

N_MIXERS = 4
HEAD_DIM = 64
N_HEADS_GROUP = D_MODEL // (N_MIXERS * HEAD_DIM)
GROUP_WIDTH = N_HEADS_GROUP * HEAD_DIM
MIX_WIDTH = N_MIXERS * GROUP_WIDTH
Q_BLK = 128
GRID_W = 64
NA_ROWS = 8
NA_COLS = 16
DIFF_QK_DIM = HEAD_DIM // 2
DIL_PATTERNS = ((128, 1), (512, 4), (2048, 16))
MLA_Q_RANK = D_MODEL // 4
MLA_KV_RANK = D_MODEL // 8
MLA_NOPE = HEAD_DIM
MLA_ROPE = HEAD_DIM // 2
MLA_V = HEAD_DIM
ROPE_THETA = 10000.0
T5_BUCKETS = 32
T5_MAX_DIST = 1024
T5_HEADS = 2 * N_HEADS_GROUP
D_FF = ((8 * D_MODEL + 3 * 256 - 1) // (3 * 256)) * 256
IN_WIDTH = 9 * GROUP_WIDTH + MLA_Q_RANK + MLA_KV_RANK + MLA_ROPE
EPS = 1e-6

kernel_name = 'hybrid_parallel_heads_encoder'


def rms_norm(x, g):
    xf = x.astype(jnp.float32)
    y = xf * lax.rsqrt(jnp.mean(xf * xf, axis=-1, keepdims=True) + EPS)
    return (y * g.astype(jnp.float32)).astype(x.dtype)


def t5_bucket(rel):
    nb = T5_BUCKETS // 2
    max_exact = nb // 2
    side = jnp.where(rel > 0, nb, 0)
    n = jnp.abs(rel)
    large = max_exact + (jnp.log(jnp.maximum(n, 1).astype(jnp.float32) / max_exact)
                         / math.log(T5_MAX_DIST / max_exact) * (nb - max_exact)).astype(jnp.int32)
    large = jnp.minimum(large, nb - 1)
    return side + jnp.where(n < max_exact, n, large)


def rope_tables(pos):
    half = MLA_ROPE // 2
    inv_freq = ROPE_THETA ** (-jnp.arange(half, dtype=jnp.float32) / half)
    ang = pos.astype(jnp.float32)[:, None] * inv_freq[None, :]
    return jnp.cos(ang), jnp.sin(ang)


def apply_rope(x, cos, sin):
    half = x.shape[-1] // 2
    x1 = x[..., :half].astype(jnp.float32)
    x2 = x[..., half:].astype(jnp.float32)
    return jnp.concatenate([x1 * cos - x2 * sin, x1 * sin + x2 * cos], axis=-1).astype(x.dtype)


def neighborhood_attention(q, k, v, rpb):
    B, S, H, D = q.shape
    rows = S // GRID_W
    wr = min(NA_ROWS, rows)
    scale = D ** -0.5
    kg = k.reshape(B, rows, GRID_W, H, D)
    vg = v.reshape(B, rows, GRID_W, H, D)
    qg = q.reshape(B, rows, GRID_W, H, D).swapaxes(0, 1)
    col = jnp.arange(GRID_W)
    col_idx = jnp.clip(col - NA_COLS // 2, 0, GRID_W - NA_COLS)[:, None] + jnp.arange(NA_COLS)[None, :]
    col_bias_idx = col_idx - col[:, None] + NA_COLS - 1

    def one_row(args):
        q_row, r = args
        r0 = jnp.clip(r - wr // 2, 0, rows - wr)
        k_rows = lax.dynamic_slice_in_dim(kg, r0, wr, axis=1)
        v_rows = lax.dynamic_slice_in_dim(vg, r0, wr, axis=1)
        k_nb = k_rows[:, :, col_idx]
        v_nb = v_rows[:, :, col_idx]
        row_bias_idx = r0 + jnp.arange(wr) - r + NA_ROWS - 1
        bias = rpb[:, row_bias_idx[None, :, None], col_bias_idx[:, None, :]]
        s = jnp.einsum('bqhd,brqchd->bhqrc', q_row, k_nb).astype(jnp.float32) * scale + bias.astype(jnp.float32)
        p = jax.nn.softmax(s.reshape(B, H, GRID_W, wr * NA_COLS), axis=-1).reshape(s.shape)
        return jnp.einsum('bhqrc,brqchd->bqhd', p.astype(v.dtype), v_nb)

    out = lax.map(one_row, (qg, jnp.arange(rows)))
    return out.swapaxes(0, 1).reshape(B, S, H * D)


def diff_attention(q, k, v, lam, lam_init, subln_g, table):
    B, S, H, _, dk = q.shape
    nblk = S // Q_BLK
    scale = dk ** -0.5
    qb = q.reshape(B, nblk, Q_BLK, H, 2, dk).swapaxes(0, 1)
    kpos = jnp.arange(S)

    def one_block(args):
        q_blk, i = args
        qpos = i * Q_BLK + jnp.arange(Q_BLK)
        bias = table[t5_bucket(kpos[None, :] - qpos[:, None])].transpose(2, 0, 1).astype(jnp.float32)
        s = jnp.einsum('bqhcd,bkhcd->bchqk', q_blk, k).astype(jnp.float32) * scale + bias[None, None]
        p = jax.nn.softmax(s, axis=-1)
        a = p[:, 0] - lam * p[:, 1]
        return jnp.einsum('bhqk,bkhd->bqhd', a.astype(v.dtype), v)

    o = lax.map(one_block, (qb, jnp.arange(nblk))).swapaxes(0, 1).reshape(B, S, H, -1)
    o = rms_norm(o, subln_g) * (1.0 - lam_init)
    return o.reshape(B, S, -1)


def dilated_attention(q, k, v, table):
    B, S, H, D = q.shape
    nblk = S // Q_BLK
    scale = D ** -0.5
    qb = q.reshape(B, nblk, Q_BLK, H, D).swapaxes(0, 1)

    def one_block(args):
        q_blk, i = args
        qpos = i * Q_BLK + jnp.arange(Q_BLK)
        outs, lses = [], []
        for window, dil in DIL_PATTERNS:
            m = window // (2 * dil)
            offs = dil * jnp.arange(-m, m + 1)
            kpos = qpos[:, None] + offs[None, :]
            valid = (kpos >= 0) & (kpos < S)
            kidx = jnp.clip(kpos, 0, S - 1)
            k_g = k[:, kidx]
            v_g = v[:, kidx]
            bias = table[t5_bucket(offs)].T.astype(jnp.float32)
            s = jnp.einsum('bqhd,bqkhd->bhqk', q_blk, k_g).astype(jnp.float32) * scale + bias[None, :, None, :]
            s = jnp.where(valid[None, None], s, -jnp.inf)
            s_max = jnp.max(s, axis=-1, keepdims=True)
            e = jnp.exp(s - s_max)
            den = jnp.sum(e, axis=-1)
            o = jnp.einsum('bhqk,bqkhd->bqhd', e, v_g.astype(jnp.float32)) / den.transpose(0, 2, 1)[..., None]
            outs.append(o)
            lses.append(s_max[..., 0] + jnp.log(den))
        wts = jax.nn.softmax(jnp.stack(lses), axis=0)
        out = jnp.sum(jnp.stack(outs) * wts.transpose(0, 1, 3, 2)[..., None], axis=0)
        return out.astype(q.dtype)

    out = lax.map(one_block, (qb, jnp.arange(nblk)))
    return out.swapaxes(0, 1).reshape(B, S, H * D)


def mla_attention(q_nope, q_rope, k_nope, k_rope, v):
    B, S, H, _ = q_nope.shape
    nblk = S // Q_BLK
    scale = (MLA_NOPE + MLA_ROPE) ** -0.5

    def blocks(t):
        return t.reshape(B, nblk, Q_BLK, *t.shape[2:]).swapaxes(0, 1)

    def one_block(args):
        qn, qr = args
        s = jnp.einsum('bqhd,bkhd->bhqk', qn, k_nope) + jnp.einsum('bqhd,bkd->bhqk', qr, k_rope)
        p = jax.nn.softmax(s.astype(jnp.float32) * scale, axis=-1)
        return jnp.einsum('bhqk,bkhd->bqhd', p.astype(v.dtype), v)

    out = lax.map(one_block, (blocks(q_nope), blocks(q_rope)))
    return out.swapaxes(0, 1).reshape(B, S, -1)


def setup_inputs(seed: int = 0) -> dict:
    key = jax.random.key(seed)
    ks = jax.random.split(key, 17)
    H = N_HEADS_GROUP

    def nrm(k, shape, scale):
        return jax.random.normal(k, shape, jnp.float32) * scale

    return {
        'x': nrm(ks[0], (BATCH, SEQ, D_MODEL), 1.0),
        'attn_norm': 1.0 + nrm(ks[1], (DEPTH, D_MODEL), 0.05),
        'w_in': nrm(ks[2], (DEPTH, D_MODEL, IN_WIDTH), D_MODEL ** -0.5),
        'na_rpb': nrm(ks[3], (DEPTH, H, 2 * NA_ROWS - 1, 2 * NA_COLS - 1), 0.1),
        'diff_lambda': nrm(ks[4], (DEPTH, 4, DIFF_QK_DIM), 0.1),
        'diff_subln': 1.0 + nrm(ks[5], (DEPTH, HEAD_DIM), 0.05),
        'mla_q_norm': 1.0 + nrm(ks[6], (DEPTH, MLA_Q_RANK), 0.05),
        'w_uq': nrm(ks[7], (DEPTH, MLA_Q_RANK, H * (MLA_NOPE + MLA_ROPE)), MLA_Q_RANK ** -0.5),
        'mla_kv_norm': 1.0 + nrm(ks[8], (DEPTH, MLA_KV_RANK), 0.05),
        'w_ukv': nrm(ks[9], (DEPTH, MLA_KV_RANK, H * (MLA_NOPE + MLA_V)), MLA_KV_RANK ** -0.5),
        't5_table': nrm(ks[10], (T5_BUCKETS, T5_HEADS), 0.1),
        'w_o': nrm(ks[11], (DEPTH, MIX_WIDTH, D_MODEL), MIX_WIDTH ** -0.5),
        'ffn_norm': 1.0 + nrm(ks[12], (DEPTH, D_MODEL), 0.05),
        'w_gate': nrm(ks[13], (DEPTH, D_MODEL, D_FF), D_MODEL ** -0.5),
        'w_up': nrm(ks[14], (DEPTH, D_MODEL, D_FF), D_MODEL ** -0.5),
        'w_down': nrm(ks[15], (DEPTH, D_FF, D_MODEL), D_FF ** -0.5),
        'final_norm': 1.0 + nrm(ks[16], (D_MODEL,), 0.05),
    }


def reference(x, attn_norm, w_in, na_rpb, diff_lambda, diff_subln, mla_q_norm, w_uq,
              mla_kv_norm, w_ukv, t5_table, w_o, ffn_norm, w_gate, w_up, w_down, final_norm):
    B, S, _ = x.shape
    H, G = N_HEADS_GROUP, GROUP_WIDTH
    cos, sin = rope_tables(jnp.arange(S))
    t5_diff = t5_table[:, :H]
    t5_dil = t5_table[:, H:]
    split_points = np.cumsum([G] * 9 + [MLA_Q_RANK, MLA_KV_RANK]).tolist()
    for l in range(DEPTH):
        h = rms_norm(x, attn_norm[l])
        proj = h @ w_in[l]
        qa, ka, va, qb, kb, vb, qc, kc, vc, cq, ckv, kr = jnp.split(proj, split_points, axis=-1)

        o_a = neighborhood_attention(qa.reshape(B, S, H, HEAD_DIM), ka.reshape(B, S, H, HEAD_DIM),
                                     va.reshape(B, S, H, HEAD_DIM), na_rpb[l])

        lam_init = 0.8 - 0.6 * math.exp(-0.3 * l)
        lq1, lk1, lq2, lk2 = [diff_lambda[l, j].astype(jnp.float32) for j in range(4)]
        lam = jnp.exp(jnp.sum(lq1 * lk1)) - jnp.exp(jnp.sum(lq2 * lk2)) + lam_init
        o_b = diff_attention(qb.reshape(B, S, H, 2, DIFF_QK_DIM), kb.reshape(B, S, H, 2, DIFF_QK_DIM),
                             vb.reshape(B, S, H, HEAD_DIM), lam, lam_init, diff_subln[l], t5_diff)

        o_c = dilated_attention(qc.reshape(B, S, H, HEAD_DIM), kc.reshape(B, S, H, HEAD_DIM),
                                vc.reshape(B, S, H, HEAD_DIM), t5_dil)

        cq = rms_norm(cq, mla_q_norm[l])
        qd = (cq @ w_uq[l]).reshape(B, S, H, MLA_NOPE + MLA_ROPE)
        q_nope = qd[..., :MLA_NOPE]
        q_rope = apply_rope(qd[..., MLA_NOPE:], cos[:, None, :], sin[:, None, :])
        ckv = rms_norm(ckv, mla_kv_norm[l])
        kvd = (ckv @ w_ukv[l]).reshape(B, S, H, MLA_NOPE + MLA_V)
        k_nope = kvd[..., :MLA_NOPE]
        v_d = kvd[..., MLA_NOPE:]
        k_rope = apply_rope(kr, cos, sin)
        o_d = mla_attention(q_nope, q_rope, k_nope, k_rope, v_d)

        mix = jnp.concatenate([o_a, o_b, o_c, o_d], axis=-1)
        x = x + mix @ w_o[l]

        h = rms_norm(x, ffn_norm[l])
        x = x + (jax.nn.silu(h @ w_gate[l]) * (h @ w_up[l])) @ w_down[l]
    return rms_norm(x, final_norm)
```

```python
import contextlib
import math
import numpy as np
import concourse.bass as bass
import concourse.mybir as mybir
from concourse.bass_utils import run_bass_kernel_spmd

F32 = mybir.dt.float32
BF16 = mybir.dt.bfloat16
AF = mybir.ActivationFunctionType
ALU = mybir.AluOpType

D_MODEL = 1024
D_FF = 2816
NFF = D_FF // 128
EPS = 1e-6
NCORES = 8


class Tok:
    __slots__ = ("name", "w", "readers", "dw", "dr")

    def __init__(self, name=""):
        self.name = name
        self.w = []
        self.readers = []
        self.dw = {}
        self.dr = {}


class DSem:
    __slots__ = ("name", "cnt", "handle")

    def __init__(self, name):
        self.name = name
        self.cnt = 0
        self.handle = None


class _Op:
    __slots__ = ("fn", "waits", "inc", "cnt", "dsem")

    def __init__(self, fn, waits, dsem=None):
        self.fn = fn
        self.waits = waits
        self.inc = False
        self.cnt = 0
        self.dsem = dsem


class Sched:
    ENGS = ("pe", "act", "dve", "pool", "sp")

    def __init__(self, nc):
        self.nc = nc
        self.ops = {e: [] for e in self.ENGS}
        self.waited_e = {e: {} for e in self.ENGS}
        self.waited_s = {e: {} for e in self.ENGS}
        self.dsems = []
        self.barrier_e = {}
        self.barrier_s = {}

    def dsem(self, name):
        d = DSem(name)
        self.dsems.append(d)
        return d

    def _collect(self, eng, reads, writes, accumulate):
        de = dict(self.barrier_e)
        ds = dict(self.barrier_s)

        def adde(lst):
            for (e2, i2) in lst:
                if de.get(e2, -1) < i2:
                    de[e2] = i2

        def adds(dct):
            for k, v in dct.items():
                if ds.get(k, 0) < v:
                    ds[k] = v

        for b in reads:
            adde(b.w)
            adds(b.dw)
        for b in writes:
            adde(b.w)
            adde(b.readers)
            adds(b.dr)
            if not accumulate:
                adds(b.dw)
        waits = []
        for e2, i2 in de.items():
            if e2 == eng and eng == "pe":
                continue
            if self.waited_e[eng].get(e2, -1) >= i2:
                continue
            if e2 == eng and i2 >= len(self.ops[eng]):
                continue
            self.waited_e[eng][e2] = i2
            self.ops[e2][i2].inc = True
            waits.append(("e", e2, i2))
        for k, v in ds.items():
            if self.waited_s[eng].get(k, 0) >= v:
                continue
            self.waited_s[eng][k] = v
            waits.append(("s", k, v))
        return waits

    def op(self, eng, fn, reads=(), writes=()):
        waits = self._collect(eng, reads, writes, False)
        idx = len(self.ops[eng])
        self.ops[eng].append(_Op(fn, waits))
        for b in reads:
            b.readers.append((eng, idx))
        for b in writes:
            b.w = [(eng, idx)]
            b.readers = []
            b.dw = {}
            b.dr = {}
        return idx

    def dma(self, eng, fn, dsem, reads=(), writes=(), accumulate=False):
        waits = self._collect(eng, reads, writes, accumulate)
        dsem.cnt += 16
        v = dsem.cnt
        self.ops[eng].append(_Op(fn, waits, dsem))
        for b in reads:
            b.dr[dsem] = v
        for b in writes:
            if not accumulate:
                b.dw = {}
            b.w = []
            b.readers = []
            b.dr = {} if not accumulate else b.dr
            b.dw[dsem] = v

    def barrier(self):
        for e in self.ENGS:
            if e == "sp":
                continue
            for i in range(len(self.ops[e]) - 1, -1, -1):
                if self.ops[e][i].dsem is None and self.ops[e][i].fn is not None:
                    self.barrier_e[e] = i
                    break
        for d in self.dsems:
            if d.cnt:
                self.barrier_s[d] = d.cnt

    def final_wait(self, eng="sp"):
        waits = []
        for d in self.dsems:
            if d.cnt and self.waited_s[eng].get(d, 0) < d.cnt:
                waits.append(("s", d, d.cnt))
                self.waited_s[eng][d] = d.cnt
        self.ops[eng].append(_Op(None, waits))

    def emit(self, stack):
        nc = self.nc
        esem = {}
        for e in self.ENGS:
            esem[e] = stack.enter_context(nc.semaphore("prog_" + e))
        for i, d in enumerate(self.dsems):
            d.handle = stack.enter_context(nc.semaphore("d%d_%s" % (i, d.name)))
        for e in self.ENGS:
            c = 0
            for o in self.ops[e]:
                if o.inc:
                    c += 1
                    o.cnt = c
        ops = self.ops

        def run(ename, eobj):
            for o in ops[ename]:
                for w in o.waits:
                    if w[0] == "e":
                        eobj.wait_ge(esem[w[1]], ops[w[1]][w[2]].cnt)
                    else:
                        eobj.wait_ge(w[1].handle, w[2])
                if o.fn is None:
                    continue
                ins = o.fn(eobj)
                if o.dsem is not None:
                    ins.then_inc(o.dsem.handle, 16)
                elif o.inc:
                    ins.then_inc(esem[ename], 1)

        with nc.Block() as block:
            @block.sync
            def _(e):
                run("sp", e)

            @block.tensor
            def _(e):
                run("pe", e)

            @block.scalar
            def _(e):
                run("act", e)

            @block.vector
            def _(e):
                run("dve", e)

            @block.gpsimd
            def _(e):
                run("pool", e)


TC = 512


def build_ffn(T, last):
    nc = bass.Bass("TRN2", target_bir_lowering=False)
    nch = T // TC
    xT = nc.dram_tensor("xT", [D_MODEL, T], F32, kind="ExternalInput").ap()
    mixT = nc.dram_tensor("mixT", [D_MODEL, T], F32, kind="ExternalInput").ap()
    wo_r = nc.dram_tensor("wo_r", [128, 8 * 1024], F32, kind="ExternalInput").ap()
    gn = nc.dram_tensor("gn", [128, 8], F32, kind="ExternalInput").ap()
    fin = nc.dram_tensor("fin", [128, 8], F32, kind="ExternalInput").ap()
    wg_r = nc.dram_tensor("wg_r", [NFF, 128, 1024], F32, kind="ExternalInput").ap()
    wu_r = nc.dram_tensor("wu_r", [NFF, 128, 1024], F32, kind="ExternalInput").ap()
    wd_r = nc.dram_tensor("wd_r", [8, 128, D_FF], F32, kind="ExternalInput").ap()
    yT = nc.dram_tensor("yT", [D_MODEL, T], F32, kind="ExternalOutput").ap()
    wo_b = nc.dram_tensor("wo_b", [128, 8 * 1024], BF16, kind="Internal").ap()
    wg_b = nc.dram_tensor("wg_b", [NFF, 128, 1024], BF16, kind="Internal").ap()
    wu_b = nc.dram_tensor("wu_b", [NFF, 128, 1024], BF16, kind="Internal").ap()
    wd_b = nc.dram_tensor("wd_b", [8, 128, D_FF], BF16, kind="Internal").ap()

    xT_v = xT.rearrange("(c p) t -> p c t", p=128)
    mixT_v = mixT.rearrange("(c p) t -> p c t", p=128)
    yT_v = yT.rearrange("(c p) t -> p c t", p=128)

    s = Sched(nc)
    with contextlib.ExitStack() as st:
        def sb(name, shape, dt):
            return st.enter_context(nc.sbuf_tensor(name, shape, dt))

        def ps(name):
            return st.enter_context(nc.psum_tensor(name, [128, 512], F32))

        wo_sb = sb("wo_sb", [128, 8 * 1024], BF16)
        gn_sb = sb("gn_sb", [128, 8], F32)
        fin_sb = sb("fin_sb", [128, 8], F32)
        ones32 = sb("ones32", [128, 128], F32)
        eps_sb = sb("eps_sb", [128, 1], F32)
        x_sb = [sb("x_sb%d" % i, [128, 8, TC], F32) for i in range(2)]
        mix_sb = [sb("mix_sb%d" % i, [128, 8, TC], BF16) for i in range(2)]
        sq_sb = sb("sq_sb", [128, 8, TC], F32)
        xn_sb = sb("xn_sb", [128, 8, TC], BF16)
        rstd_sb = sb("rstd_sb", [128, TC], F32)
        h_sb = sb("h_sb", [128, NFF, TC], BF16)
        sg_sb = [sb("sg_sb%d" % i, [128, TC], F32) for i in range(2)]
        NWG = 3
        wg_sb = [sb("wg_sb%d" % i, [128, 1024], BF16) for i in range(NWG)]
        wu_sb = [sb("wu_sb%d" % i, [128, 1024], BF16) for i in range(NWG)]
        wd_sb = [sb("wd_sb%d" % i, [128, D_FF], BF16) for i in range(2)]
        y_sb = sb("y_sb", [128, 8, TC], F32)
        ps_a = [ps("ps_a%d" % i) for i in range(2)]
        ps_g = [ps("ps_g%d" % i) for i in range(2)]
        ps_u = [ps("ps_u%d" % i) for i in range(2)]
        ps_ss = ps("ps_ss")

        t_wo_b, t_wg_b, t_wu_b, t_wd_b = Tok("wo_b"), Tok("wg_b"), Tok("wu_b"), Tok("wd_b")
        t_wo, t_gn, t_ones = Tok("wo"), Tok("gn"), Tok("ones")
        t_x = [[Tok("x%d_%d" % (i, j)) for j in range(8)] for i in range(2)]
        t_mix = [Tok("mix%d" % i) for i in range(2)]
        t_sq = [Tok("sq%d" % j) for j in range(8)]
        t_xn = [Tok("xn%d" % j) for j in range(8)]
        t_rstd = Tok("rstd")
        t_h = [Tok("h%d" % j) for j in range(NFF)]
        t_sg = [Tok("sg%d" % i) for i in range(2)]
        t_wg = [Tok("wg%d" % i) for i in range(NWG)]
        t_wu = [Tok("wu%d" % i) for i in range(NWG)]
        t_wd = [Tok("wd%d" % i) for i in range(2)]
        t_y = [Tok("y%d" % j) for j in range(8)]
        t_psa = [Tok("psa%d" % i) for i in range(2)]
        t_psg = [Tok("psg%d" % i) for i in range(2)]
        t_psu = [Tok("psu%d" % i) for i in range(2)]
        t_pss = Tok("pss")
        t_out = Tok("out")

        d_const = s.dsem("const")
        d_const2 = s.dsem("const2")
        d_x = [s.dsem("x%d" % i) for i in range(2)]
        d_mix = [s.dsem("mix%d" % i) for i in range(2)]
        d_wg = [s.dsem("wg%d" % i) for i in range(NWG)]
        d_wu = [s.dsem("wu%d" % i) for i in range(NWG)]
        d_wd = [s.dsem("wd%d" % i) for i in range(2)]
        d_out = s.dsem("out")

        d_cvs = [s.dsem("cv%d" % i) for i in range(4)]
        wo_b2 = wo_b.rearrange("p (c n) -> (p c) n", n=1024)
        wo_r2 = wo_r.rearrange("p (c n) -> (p c) n", n=1024)
        for q in range(8):
            s.dma("pool", lambda e, q=q: e.dma_start(out=wo_b2[q * 128:(q + 1) * 128, :],
                                                     in_=wo_r2[q * 128:(q + 1) * 128, :]), d_cvs[0],
                  writes=[t_wo_b], accumulate=True)
        for fi in range(NFF):
            s.dma("pool", lambda e, fi=fi: e.dma_start(out=wg_b[fi], in_=wg_r[fi]), d_cvs[1],
                  writes=[t_wg_b], accumulate=True)
            s.dma("pool", lambda e, fi=fi: e.dma_start(out=wu_b[fi], in_=wu_r[fi]), d_cvs[2],
                  writes=[t_wu_b], accumulate=True)
        wd_b2 = wd_b.rearrange("j p (a n) -> j (p a) n", n=1408)
        wd_r2 = wd_r.rearrange("j p (a n) -> j (p a) n", n=1408)
        for j in range(8):
            for hh in range(2):
                s.dma("pool", lambda e, j=j, hh=hh: e.dma_start(out=wd_b2[j, hh * 128:(hh + 1) * 128, :],
                                                                in_=wd_r2[j, hh * 128:(hh + 1) * 128, :]),
                      d_cvs[3], writes=[t_wd_b], accumulate=True)
        s.dma("sp", lambda e: e.dma_start(out=gn_sb[:], in_=gn), d_const, writes=[t_gn], accumulate=True)
        s.dma("sp", lambda e: e.dma_start(out=fin_sb[:], in_=fin), d_const, writes=[t_gn], accumulate=True)
        s.dma("sp", lambda e: e.dma_start(out=wo_sb[:], in_=wo_b), d_const2, reads=[t_wo_b], writes=[t_wo])
        s.op("dve", lambda e: e.memset(ones32[:], 1.0 / D_MODEL), writes=[t_ones])
        s.op("dve", lambda e: e.memset(eps_sb[:], EPS), writes=[t_ones])

        def load_chunk(ci):
            sl = ci % 2
            c0 = ci * TC
            s.dma("sp", lambda e: e.dma_start(out=x_sb[sl][:], in_=xT_v[:, :, c0:c0 + TC]), d_x[sl],
                  writes=t_x[sl])
            s.dma("pool", lambda e: e.dma_start(out=mix_sb[sl][:], in_=mixT_v[:, :, c0:c0 + TC]), d_mix[sl],
                  writes=[t_mix[sl]])

        def rms(src_sb, src_toks, gain_sb, dst_fn, dst_toks):
            for c in range(8):
                s.op("act", lambda e, c=c: e.activation(out=sq_sb[:, c, :], in_=src_sb[:, c, :], func=AF.Square),
                     reads=[src_toks[c]], writes=[t_sq[c]])
            for c in range(8):
                s.op("pe", lambda e, c=c: e.matmul(ps_ss[:], lhsT=ones32[:], rhs=sq_sb[:, c, :],
                                                    start=(c == 0), stop=(c == 7)),
                     reads=[t_ones, t_sq[c]], writes=[t_pss])
            s.op("act", lambda e: e.activation(out=rstd_sb[:], in_=ps_ss[:], func=AF.Sqrt, bias=eps_sb[:, 0:1]),
                 reads=[t_pss, t_ones], writes=[t_rstd])
            s.op("dve", lambda e: e.reciprocal(out=rstd_sb[:], in_=rstd_sb[:]),
                 reads=[t_rstd], writes=[t_rstd])
            for c in range(8):
                s.op("dve", lambda e, c=c: e.scalar_tensor_tensor(out=dst_fn(c), in0=src_sb[:, c, :],
                                                                  scalar=gain_sb[:, c:c + 1], in1=rstd_sb[:],
                                                                  op0=ALU.mult, op1=ALU.mult),
                     reads=[src_toks[c], t_rstd, t_gn], writes=[dst_toks[c]])

        wcnt = [0, 0]
        load_chunk(0)
        def do_chunk(ci):
            sl = ci % 2
            xs = x_sb[sl]
            tx = t_x[sl]
            for j in range(8):
                pb = j % 2
                for c in range(8):
                    s.op("pe", lambda e, j=j, c=c, pb=pb: e.matmul(
                        ps_a[pb][:], lhsT=wo_sb[:, c * 1024 + j * 128: c * 1024 + (j + 1) * 128],
                        rhs=mix_sb[sl][:, c, :], start=(c == 0), stop=(c == 7)),
                        reads=[t_wo, t_mix[sl]], writes=[t_psa[pb]])
                s.op("dve", lambda e, j=j, pb=pb: e.tensor_tensor(out=xs[:, j, :], in0=xs[:, j, :],
                                                                  in1=ps_a[pb][:], op=ALU.add),
                     reads=[t_psa[pb], tx[j]], writes=[tx[j]])
            rms(xs, tx, gn_sb, lambda c: xn_sb[:, c, :], t_xn)
            for fi in range(NFF):
                ws = wcnt[0] % NWG
                wcnt[0] += 1
                pb = fi % 2
                s.dma("sp", lambda e, fi=fi, ws=ws: e.dma_start(out=wg_sb[ws][:], in_=wg_b[fi]), d_wg[ws],
                      reads=[t_wg_b], writes=[t_wg[ws]])
                s.dma("sp", lambda e, fi=fi, ws=ws: e.dma_start(out=wu_sb[ws][:], in_=wu_b[fi]), d_wu[ws],
                      reads=[t_wu_b], writes=[t_wu[ws]])
                for c in range(8):
                    s.op("pe", lambda e, c=c, ws=ws, pb=pb: e.matmul(
                        ps_g[pb][:], lhsT=wg_sb[ws][:, c * 128:(c + 1) * 128], rhs=xn_sb[:, c, :],
                        start=(c == 0), stop=(c == 7)),
                        reads=[t_wg[ws], t_xn[c]], writes=[t_psg[pb]])
                for c in range(8):
                    s.op("pe", lambda e, c=c, ws=ws, pb=pb: e.matmul(
                        ps_u[pb][:], lhsT=wu_sb[ws][:, c * 128:(c + 1) * 128], rhs=xn_sb[:, c, :],
                        start=(c == 0), stop=(c == 7)),
                        reads=[t_wu[ws], t_xn[c]], writes=[t_psu[pb]])
                s.op("act", lambda e, pb=pb: e.activation(out=sg_sb[pb][:], in_=ps_g[pb][:], func=AF.Silu),
                     reads=[t_psg[pb]], writes=[t_sg[pb]])
                s.op("dve", lambda e, fi=fi, pb=pb: e.tensor_tensor(out=h_sb[:, fi, :], in0=sg_sb[pb][:],
                                                                    in1=ps_u[pb][:], op=ALU.mult),
                     reads=[t_sg[pb], t_psu[pb]], writes=[t_h[fi]])
            for j in range(8):
                ws = wcnt[1] % 2
                wcnt[1] += 1
                pb = j % 2
                s.dma("sp", lambda e, j=j, ws=ws: e.dma_start(out=wd_sb[ws][:], in_=wd_b[j]), d_wd[ws],
                      reads=[t_wd_b], writes=[t_wd[ws]])
                for fi in range(NFF):
                    s.op("pe", lambda e, fi=fi, ws=ws, pb=pb: e.matmul(
                        ps_a[pb][:], lhsT=wd_sb[ws][:, fi * 128:(fi + 1) * 128], rhs=h_sb[:, fi, :],
                        start=(fi == 0), stop=(fi == NFF - 1)),
                        reads=[t_wd[ws], t_h[fi]], writes=[t_psa[pb]])
                s.op("dve", lambda e, j=j, pb=pb: e.tensor_tensor(out=xs[:, j, :], in0=xs[:, j, :],
                                                                  in1=ps_a[pb][:], op=ALU.add),
                     reads=[t_psa[pb], tx[j]], writes=[tx[j]])
            c0 = ci * TC
            if last:
                rms(xs, tx, fin_sb, lambda c: y_sb[:, c, :], t_y)
                s.dma("sp", lambda e, c0=c0: e.dma_start(out=yT_v[:, :, c0:c0 + TC], in_=y_sb[:]), d_out,
                      reads=t_y, writes=[t_out], accumulate=True)
            else:
                s.dma("sp", lambda e, c0=c0, xs=xs: e.dma_start(out=yT_v[:, :, c0:c0 + TC], in_=xs[:]), d_out,
                      reads=tx, writes=[t_out], accumulate=True)
        for ci in range(nch):
            if ci + 1 < nch:
                load_chunk(ci + 1)
            do_chunk(ci)
        s.final_wait("sp")
        s.emit(st)
    return nc


def prep_ffn_weights(w_o, ffn_norm, w_gate, w_up, w_down, final_norm):
    wo_r = np.ascontiguousarray(w_o.reshape(8, 128, 1024).transpose(1, 0, 2)).reshape(128, 8 * 1024)
    gn = np.ascontiguousarray(ffn_norm.reshape(8, 128).T)
    fin = np.ascontiguousarray(final_norm.reshape(8, 128).T)
    wg_r = np.ascontiguousarray(w_gate.reshape(8, 128, NFF, 128).transpose(2, 1, 0, 3)).reshape(NFF, 128, 1024)
    wu_r = np.ascontiguousarray(w_up.reshape(8, 128, NFF, 128).transpose(2, 1, 0, 3)).reshape(NFF, 128, 1024)
    wd_r = np.ascontiguousarray(w_down.reshape(NFF, 128, 8, 128).transpose(2, 1, 0, 3)).reshape(8, 128, D_FF)
    return dict(wo_r=wo_r, gn=gn, fin=fin, wg_r=wg_r, wu_r=wu_r, wd_r=wd_r)


NW = 1152
QB = 512
NEG = -30000.0
DIL_PATTERNS = ((128, 1), (512, 4), (2048, 16))


def dil_deltas():
    out = []
    for pi, (window, dil) in enumerate(DIL_PATTERNS):
        m = window // (2 * dil)
        lo, hi = -m * dil - 127, m * dil + 511
        d = -(-lo // 128) * 128
        while d <= hi:
            out.append((pi, d))
            d += 128
    return out


DIL_DELTAS = dil_deltas()
DIFF_DELTAS = list(range(-640, 1025, 128))


def na_tile_ids(S):
    nqc, nkt = S // QB, S // 128
    gen, ids = [], {}
    for u in range(8):
        gen.append((1, 2 + u))
    for qc in range(nqc):
        for kt in range(max(0, 4 * qc - 2), min(nkt, 4 * qc + 6)):
            if 0 < qc < nqc - 1:
                ids[(qc, kt)] = kt - (4 * qc - 2)
            else:
                ids[(qc, kt)] = len(gen)
                gen.append((qc, kt))
    return ids, gen


def build_attn(S, layer):
    nc = bass.Bass("TRN2", target_bir_lowering=False)
    nqc, nkt = S // QB, S // 128
    lam_init = 0.8 - 0.6 * math.exp(-0.3 * layer)
    na_ids, na_gen = na_tile_ids(S)
    n_na = len(na_gen)

    def din(name, shape, dt=F32):
        return nc.dram_tensor(name, shape, dt, kind="ExternalInput").ap()

    xT = din("xT", [D_MODEL, S])
    gnA = din("gnA", [128, 8])
    win_r = din("win_r", [128, 8 * NW])
    gq = din("gq", [128, 2])
    gkv = din("gkv", [128, 1])
    wuq_r = din("wuq_r", [128, 2 * 192])
    wukv_r = din("wukv_r", [128, 128])
    rope_tab = din("rope_tab", [2, 32, S])
    lam_in = din("lam_in", [64, 128])
    subln = din("subln", [64, 1])
    cst_diff = din("cst_diff", [128, 2])
    bias_na = din("bias_na", [n_na, 128, QB])
    bias_diff = din("bias_diff", [len(DIFF_DELTAS), 128, QB])
    bias_dil = din("bias_dil", [len(DIL_DELTAS), 128, QB])
    oT = nc.dram_tensor("oT", [4, 64, S], F32, kind="ExternalOutput").ap()
    qk_scr = nc.dram_tensor("qk_scr", [5, 128, S], BF16, kind="Internal").ap()
    v_scr = nc.dram_tensor("v_scr", [4, 128, nkt * 64], BF16, kind="Internal").ap()
    xT_v = xT.rearrange("(c p) t -> p c t", p=128)

    s = Sched(nc)
    with contextlib.ExitStack() as st:
        def sbx(stack, name, shape, dt):
            return stack.enter_context(nc.sbuf_tensor(name, shape, dt))

        PS = [st.enter_context(nc.psum_tensor("ps%d" % i, [128, 512], F32)) for i in range(8)]
        t_PS = [Tok("ps%d" % i) for i in range(8)]
        t_scr = [Tok("scr%d" % i) for i in range(5)]
        t_vscr = Tok("vscr")
        t_out = Tok("out")
        d_out = s.dsem("out")

        ones32 = sbx(st, "ones32", [128, 128], F32)
        onesq = sbx(st, "onesq", [128, 128], F32)
        onesk = sbx(st, "onesk", [128, 128], F32)
        ones64 = sbx(st, "ones64", [64, 64], F32)
        sel32 = sbx(st, "sel32", [65, 64], F32)
        eps_sb = sbx(st, "eps_sb", [128, 1], F32)
        eps2_sb = sbx(st, "eps2_sb", [128, 1], F32)
        gnA_sb = sbx(st, "gnA_sb", [128, 8], F32)
        gq_sb = sbx(st, "gq_sb", [128, 2], F32)
        gkv_sb = sbx(st, "gkv_sb", [128, 1], F32)
        lam_sb = sbx(st, "lam_sb", [64, 128], F32)
        lamw = sbx(st, "lamw", [64, 64], F32)
        lams = sbx(st, "lams", [64, 4], F32)
        subln_sb = sbx(st, "subln_sb", [64, 1], F32)
        cst_sb = sbx(st, "cst_sb", [128, 2], F32)
        t_c = Tok("consts")
        t_lam = Tok("lam")
        cmul = 1.0 - lam_init
        dcs = [s.dsem("c%d" % i) for i in range(7)]
        for i, (dst, src) in enumerate([(gnA_sb, gnA), (gq_sb, gq), (gkv_sb, gkv), (lam_sb, lam_in),
                                        (subln_sb, subln), (cst_sb, cst_diff)]):
            s.dma("sp", lambda e, dst=dst, src=src: e.dma_start(out=dst[:], in_=src), dcs[i],
                  writes=[t_c], accumulate=True)
        for tile_, val in [(ones32, 1.0 / 1024), (onesq, 1.0 / 256), (onesk, 1.0 / 128), (ones64, 1.0 / 64),
                           (sel32, 0.0), (eps_sb, EPS), (eps2_sb, EPS / (cmul * cmul))]:
            s.op("dve", lambda e, tile_=tile_, val=val: e.memset(tile_[:], val), writes=[t_c])
        s.op("dve", lambda e: e.memset(sel32[64:65, :], 1.0), writes=[t_c])
        s.op("dve", lambda e: e.tensor_tensor(out=lamw[:, 0:32], in0=lam_sb[:, 0:32], in1=lam_sb[:, 32:64],
                                              op=ALU.mult), reads=[t_c], writes=[t_lam])
        s.op("dve", lambda e: e.tensor_tensor(out=lamw[:, 32:64], in0=lam_sb[:, 64:96], in1=lam_sb[:, 96:128],
                                              op=ALU.mult), reads=[t_c, t_lam], writes=[t_lam])
        s.op("dve", lambda e: e.reduce_sum(out=lams[:, 0:1], in_=lamw[:, 0:32], axis=mybir.AxisListType.X),
             reads=[t_lam], writes=[t_lam])
        s.op("dve", lambda e: e.reduce_sum(out=lams[:, 1:2], in_=lamw[:, 32:64], axis=mybir.AxisListType.X),
             reads=[t_lam], writes=[t_lam])
        s.op("act", lambda e: e.activation(out=lams[:, 2:4], in_=lams[:, 0:2], func=AF.Exp),
             reads=[t_lam], writes=[t_lam])
        s.op("dve", lambda e: e.tensor_tensor(out=lams[:, 0:1], in0=lams[:, 3:4], in1=lams[:, 2:3],
                                              op=ALU.subtract), reads=[t_lam], writes=[t_lam])
        s.op("dve", lambda e: e.tensor_scalar(out=lams[:, 1:2], in0=lams[:, 0:1], scalar1=-lam_init,
                                              scalar2=None, op0=ALU.add), reads=[t_lam], writes=[t_lam])
        nlam = lams[:, 1:2]

        with contextlib.ExitStack() as p1:
            win_sb = sbx(p1, "win_sb", [128, 8 * NW], BF16)
            wuq_sb = sbx(p1, "wuq_sb", [128, 2 * 192], BF16)
            wukv_sb = sbx(p1, "wukv_sb", [128, 128], BF16)
            x_sb = [sbx(p1, "x_sb%d" % i, [128, 8, QB], F32) for i in range(2)]
            sq_sb = sbx(p1, "sq_sb", [128, 8, QB], F32)
            xn_sb = sbx(p1, "xn_sb", [128, 8, QB], BF16)
            rstd_sb = sbx(p1, "rstd_sb", [128, QB], F32)
            stg = [sbx(p1, "stg%d" % i, [128, QB], BF16) for i in range(3)]
            stgq = sbx(p1, "stgq", [128, QB], BF16)
            stgk = sbx(p1, "stgk", [128, QB], BF16)
            vstg = sbx(p1, "vstg", [128, 4, 4, 64], BF16)
            sqq = sbx(p1, "sqq", [128, 2, QB], F32)
            cqn = sbx(p1, "cqn", [128, 2, QB], BF16)
            ckvn = sbx(p1, "ckvn", [128, QB], BF16)
            rstd2 = sbx(p1, "rstd2", [128, QB], F32)
            tab = [sbx(p1, "tab%d" % i, [96, 2, QB], F32) for i in range(2)]
            rt1 = sbx(p1, "rt1", [96, QB], F32)
            rt2 = sbx(p1, "rt2", [96, QB], F32)

            t_w = Tok("w")
            d_w = [s.dsem("w%d" % i) for i in range(3)]
            win_r2 = win_r.rearrange("p (c n) -> p c n", n=NW)
            win_s2 = win_sb[:].rearrange("p (c n) -> p c n", n=NW)
            for c in range(8):
                s.dma("pool", lambda e, c=c: e.dma_start(out=win_s2[:, c, :], in_=win_r2[:, c, :]), d_w[0],
                      writes=[t_w], accumulate=True)
            s.dma("pool", lambda e: e.dma_start(out=wuq_sb[:], in_=wuq_r), d_w[1], writes=[t_w], accumulate=True)
            s.dma("pool", lambda e: e.dma_start(out=wukv_sb[:], in_=wukv_r), d_w[2], writes=[t_w], accumulate=True)

            t_x = [[Tok() for _ in range(8)] for _ in range(2)]
            t_tab = [Tok(), Tok()]
            d_x = [s.dsem("x0"), s.dsem("x1")]
            d_tab = [s.dsem("t0"), s.dsem("t1")]
            t_sq = [Tok() for _ in range(8)]
            t_xn = [Tok() for _ in range(8)]
            t_rstd, t_rstd2 = Tok(), Tok()
            t_stg = [Tok() for _ in range(3)]
            d_stg = [s.dsem("stg%d" % i) for i in range(3)]
            t_stgq, t_stgk, t_vstg = Tok(), Tok(), Tok()
            d_stgq, d_stgk, d_vstg = s.dsem("stgq"), s.dsem("stgk"), s.dsem("vstg")
            t_sqq, t_cqn, t_ckvn, t_rt1, t_rt2 = Tok(), [Tok(), Tok()], Tok(), Tok(), Tok()
            rope_v = rope_tab.rearrange("a r t -> r a t")

            def load_chunk(ci):
                sl = ci % 2
                c0 = ci * QB
                s.dma("sp", lambda e: e.dma_start(out=x_sb[sl][:], in_=xT_v[:, :, c0:c0 + QB]), d_x[sl],
                      writes=t_x[sl])
                s.dma("sp", lambda e: e.dma_start(out=tab[sl][64:96, :, :], in_=rope_v[:, :, c0:c0 + QB]),
                      d_tab[sl], writes=[t_tab[sl]])

            def stat(src_fn, n, ones_t, src_toks, sq_fn, sq_toks, eps_t, rstd_t, rstd_tok, scale=1.0):
                for c in range(n):
                    s.op("act", lambda e, c=c: e.activation(out=sq_fn(c), in_=src_fn(c), func=AF.Square),
                         reads=[src_toks[c]], writes=[sq_toks[c]])
                for c in range(n):
                    s.op("pe", lambda e, c=c: e.matmul(PS[0][:], lhsT=ones_t[:], rhs=sq_fn(c),
                                                        start=(c == 0), stop=(c == n - 1)),
                         reads=[t_c, sq_toks[c]], writes=[t_PS[0]])
                s.op("act", lambda e: e.activation(out=rstd_t[:], in_=PS[0][:], func=AF.Sqrt,
                                                   bias=eps_t[:, 0:1], scale=scale),
                     reads=[t_PS[0], t_c], writes=[rstd_tok])
                s.op("dve", lambda e: e.reciprocal(out=rstd_t[:], in_=rstd_t[:]), reads=[rstd_tok],
                     writes=[rstd_tok])

            def do_chunk(ci):
                sl = ci % 2
                c0 = ci * QB
                xs, tx = x_sb[sl], t_x[sl]
                tb, ttb = tab[sl], t_tab[sl]
                stat(lambda c: xs[:, c, :], 8, ones32, tx, lambda c: sq_sb[:, c, :], t_sq, eps_sb, rstd_sb, t_rstd)
                for c in range(8):
                    s.op("dve", lambda e, c=c: e.scalar_tensor_tensor(
                        out=xn_sb[:, c, :], in0=xs[:, c, :], scalar=gnA_sb[:, c:c + 1], in1=rstd_sb[:],
                        op0=ALU.mult, op1=ALU.mult), reads=[tx[c], t_rstd, t_c], writes=[t_xn[c]])

                def proj(ps_i, col0, ncols, rows=None):
                    for c in range(8):
                        s.op("pe", lambda e, c=c: e.matmul(
                            PS[ps_i][0:ncols, :], lhsT=win_s2[:, c, col0:col0 + ncols], rhs=xn_sb[:, c, :],
                            start=(c == 0), stop=(c == 7)), reads=[t_w, t_xn[c]], writes=[t_PS[ps_i]])

                for g in range(3):
                    pi = 1 + g % 2
                    proj(pi, g * 128, 128)
                    s.op("act", lambda e, g=g, pi=pi: e.copy(out=stg[g][:], in_=PS[pi][:]),
                         reads=[t_PS[pi]], writes=[t_stg[g]])
                    s.dma("sp", lambda e, g=g: e.dma_start(out=qk_scr[g, :, c0:c0 + QB], in_=stg[g][:]), d_stg[g],
                          reads=[t_stg[g]], writes=[t_scr[g]], accumulate=True)
                for sub in range(4):
                    pi = 3 + sub // 2
                    o0 = (sub % 2) * 192
                    for c in range(8):
                        s.op("pe", lambda e, c=c, sub=sub, pi=pi, o0=o0: e.matmul(
                            PS[pi][:, o0:o0 + 192], lhsT=xn_sb[:, c, sub * 128:(sub + 1) * 128],
                            rhs=win_s2[:, c, 384:576], start=(c == 0), stop=(c == 7)),
                            reads=[t_w, t_xn[c]], writes=[t_PS[pi]])
                proj(5, 576, 128)
                proj(6, 704, 128)
                proj(7, 832, 128)
                stat(lambda c: PS[5 + c][:], 2, onesq, [t_PS[5], t_PS[6]], lambda c: sqq[:, c, :],
                     [t_sqq, t_sqq], eps_sb, rstd2, t_rstd2)
                for c in range(2):
                    s.op("dve", lambda e, c=c: e.scalar_tensor_tensor(
                        out=cqn[:, c, :], in0=PS[5 + c][:], scalar=gq_sb[:, c:c + 1], in1=rstd2[:],
                        op0=ALU.mult, op1=ALU.mult), reads=[t_PS[5 + c], t_rstd2, t_c], writes=[t_cqn[c]])
                for ab in range(2):
                    for c in range(2):
                        s.op("pe", lambda e, c=c, ab=ab: e.matmul(
                            PS[1 + ab][0:96, :], lhsT=wuq_sb[:, c * 192 + ab * 96:c * 192 + ab * 96 + 96],
                            rhs=cqn[:, c, :], start=(c == 0), stop=(c == 1)),
                            reads=[t_w, t_cqn[c]], writes=[t_PS[1 + ab]])

                def rope_to(stage, t_stage, psA, psB, nope_ps):
                    s.op("dve", lambda e: e.tensor_tensor(out=rt1[64:96, :], in0=PS[psA][64:96, :],
                                                          in1=tb[64:96, 0, :], op=ALU.mult),
                         reads=[t_PS[psA], ttb], writes=[t_rt1])
                    s.op("dve", lambda e: e.tensor_tensor(out=rt2[64:96, :], in0=PS[psB][64:96, :],
                                                          in1=tb[64:96, 1, :], op=ALU.mult),
                         reads=[t_PS[psB], ttb], writes=[t_rt2])
                    s.op("dve", lambda e: e.tensor_tensor(out=stage[64:96, :], in0=rt1[64:96, :],
                                                          in1=rt2[64:96, :], op=ALU.add),
                         reads=[t_rt1, t_rt2], writes=[t_stage])
                    s.op("act", lambda e: e.copy(out=stage[0:64, :], in_=PS[nope_ps][0:64, :]),
                         reads=[t_PS[nope_ps]], writes=[t_stage])

                rope_to(stgq, t_stgq, 1, 2, 1)
                s.dma("sp", lambda e: e.dma_start(out=qk_scr[3, 0:96, c0:c0 + QB], in_=stgq[0:96, :]), d_stgq,
                      reads=[t_stgq], writes=[t_scr[3]], accumulate=True)
                stat(lambda c: PS[7][:], 1, onesk, [t_PS[7]], lambda c: sqq[:, 0, :], [t_sqq], eps_sb, rstd2,
                     t_rstd2)
                s.op("dve", lambda e: e.scalar_tensor_tensor(
                    out=ckvn[:], in0=PS[7][:], scalar=gkv_sb[:, 0:1], in1=rstd2[:],
                    op0=ALU.mult, op1=ALU.mult), reads=[t_PS[7], t_rstd2, t_c], writes=[t_ckvn])
                s.op("pe", lambda e: e.matmul(PS[1][0:64, :], lhsT=wukv_sb[:, 0:64], rhs=ckvn[:],
                                              start=True, stop=True), reads=[t_w, t_ckvn], writes=[t_PS[1]])
                proj(2, 960, 96)
                proj(5, 1056, 96)
                rope_to(stgk, t_stgk, 2, 5, 1)
                s.dma("sp", lambda e: e.dma_start(out=qk_scr[4, 0:96, c0:c0 + QB], in_=stgk[0:96, :]), d_stgk,
                      reads=[t_stgk], writes=[t_scr[4]], accumulate=True)
                for sub in range(4):
                    s.op("pe", lambda e, sub=sub: e.matmul(
                        PS[6][:, sub * 64:(sub + 1) * 64], lhsT=ckvn[:, sub * 128:(sub + 1) * 128],
                        rhs=wukv_sb[:, 64:128], start=True, stop=True),
                        reads=[t_w, t_ckvn], writes=[t_PS[6]])
                for m in range(3):
                    for half in range(2):
                        src = PS[3 + half][:, 0:384].rearrange("p (a n) -> p a n", n=192)[:, :, m * 64:(m + 1) * 64]
                        s.op("act", lambda e, m=m, half=half, src=src: e.copy(
                            out=vstg[:, m, 2 * half:2 * half + 2, :], in_=src),
                            reads=[t_PS[3 + half]], writes=[t_vstg])
                s.op("act", lambda e: e.copy(out=vstg[:, 3, :, :],
                                             in_=PS[6][:, 0:256].rearrange("p (a n) -> p a n", n=64)),
                     reads=[t_PS[6]], writes=[t_vstg])
                s.dma("sp", lambda e: e.dma_start(
                    out=v_scr[:, :, ci * 256:(ci + 1) * 256].rearrange("m p n -> p m n"),
                    in_=vstg[:].rearrange("p m a n -> p m (a n)")), d_vstg,
                    reads=[t_vstg], writes=[t_vscr], accumulate=True)

            load_chunk(0)
            for ci in range(nqc):
                if ci + 1 < nqc:
                    load_chunk(ci + 1)
                do_chunk(ci)
        s.barrier()

        with contextlib.ExitStack() as p2:
            KT_sb = sbx(p2, "KT_sb", [96, S], BF16)
            V_sb = sbx(p2, "V_sb", [128, nkt, 65], BF16)
            nbias = max(n_na, len(DIL_DELTAS))
            bias_sb = sbx(p2, "bias_sb", [128, nbias, QB], F32)
            Q_sb = [sbx(p2, "Q_sb%d" % i, [96, QB], BF16) for i in range(2)]
            NP, NS, NT = 3, 4, 2
            PT = [sbx(p2, "PT%d" % i, [128, QB], BF16) for i in range(NP)]
            tmp = [sbx(p2, "tmp%d" % i, [128, QB], F32) for i in range(NT)]
            o_sb = [sbx(p2, "o_sb%d" % i, [65, QB], F32) for i in range(2)]
            rden = sbx(p2, "rden", [64, QB], F32)
            on_sb = [sbx(p2, "on_sb%d" % i, [64, QB], F32) for i in range(2)]
            od_sb = sbx(p2, "od_sb", [64, QB], F32)
            sq2 = sbx(p2, "sq2", [64, QB], F32)
            r2 = sbx(p2, "r2", [64, QB], F32)
            ost = [sbx(p2, "ost%d" % i, [64, QB], F32) for i in range(2)]

            t_KT, t_V, t_bias = Tok("KT"), Tok("V"), Tok("bias")
            d_KT, d_V, d_bias = s.dsem("KT"), s.dsem("V"), s.dsem("bias")
            t_Q = [Tok(), Tok()]
            d_Q = [s.dsem("Q0"), s.dsem("Q1")]
            t_PT = [Tok() for _ in range(NP)]
            t_tmp = [Tok() for _ in range(NT)]
            t_o = [Tok(), Tok()]
            t_rden, t_on, t_od, t_sq2, t_r2 = Tok(), [Tok(), Tok()], Tok(), Tok(), Tok()
            t_ost = [Tok(), Tok()]
            d_ost = [s.dsem("ost0"), s.dsem("ost1")]
            S_BANKS = [0, 1, 2, 3]
            ACC = [4, 5]
            PX = 6
            s.op("dve", lambda e: e.memset(V_sb[:, :, 64:65], 1.0), writes=[t_V])
            gctr = [0]
            octr = [0]

            def run_mixer(m, q_scr, q_r0, k_scr, k_r0, R, nmaps, scale, entries_fn, bias_src, nb, is_diff):
                ncol = 8
                cw = S // ncol
                for cb in range(ncol):
                    s.dma("sp", lambda e, cb=cb: e.dma_start(
                        out=KT_sb[0:R, cb * cw:(cb + 1) * cw],
                        in_=qk_scr[k_scr, k_r0:k_r0 + R, cb * cw:(cb + 1) * cw]), d_KT,
                        reads=[t_scr[k_scr]], writes=[t_KT], accumulate=(cb > 0))
                vsrc = v_scr[m].rearrange("p (k n) -> p k n", n=64)
                nvb = 8
                kw = nkt // nvb
                for vb in range(nvb):
                    s.dma("sp", lambda e, vb=vb: e.dma_start(
                        out=V_sb[:, vb * kw:(vb + 1) * kw, 0:64], in_=vsrc[:, vb * kw:(vb + 1) * kw, :]), d_V,
                        reads=[t_vscr], writes=[t_V], accumulate=True)
                if bias_src is not None:
                    for bi in range(nb):
                        s.dma("sp", lambda e, bi=bi: e.dma_start(out=bias_sb[:, bi, :], in_=bias_src[bi]), d_bias,
                              writes=[t_bias], accumulate=(bi > 0))
                rows = R // nmaps

                def load_q(qc):
                    sl = qc % 2
                    s.dma("sp", lambda e: e.dma_start(
                        out=Q_sb[sl][0:R, :], in_=qk_scr[q_scr, q_r0:q_r0 + R, qc * QB:(qc + 1) * QB]), d_Q[sl],
                        reads=[t_scr[q_scr]], writes=[t_Q[sl]])

                ents = []
                for qc in range(nqc):
                    el = entries_fn(qc)
                    for i, (mp, kt, bk, bi) in enumerate(el):
                        first = all(e2[0] != mp for e2 in el[:i])
                        lastm = all(e2[0] != mp for e2 in el[i + 1:])
                        ents.append((qc, mp, kt, bk, bi, first, lastm, i == len(el) - 1))
                n = len(ents)

                def rec_qk(i):
                    qc, mp, kt, bk, bi, first, lastm, lastc = ents[i]
                    g = gctr[0] + i
                    bank = S_BANKS[g % NS]
                    r0 = mp * rows
                    s.op("pe", lambda e: e.matmul(
                        PS[bank][:], lhsT=KT_sb[r0:r0 + rows, kt * 128:(kt + 1) * 128],
                        rhs=Q_sb[qc % 2][r0:r0 + rows, :], start=True, stop=True),
                        reads=[t_KT, t_Q[qc % 2]], writes=[t_PS[bank]])

                def rec_exp_pv(i):
                    qc, mp, kt, bk, bi, first, lastm, lastc = ents[i]
                    g = gctr[0] + i
                    bank = S_BANKS[g % NS]
                    pt = g % NP
                    if bk == "tile":
                        tt = g % NT
                        s.op("dve", lambda e: e.scalar_tensor_tensor(
                            out=tmp[tt][:], in0=PS[bank][:], scalar=scale, in1=bias_sb[:, bi, :],
                            op0=ALU.mult, op1=ALU.add), reads=[t_PS[bank], t_bias], writes=[t_tmp[tt]])
                        s.op("act", lambda e: e.activation(out=PT[pt][:], in_=tmp[tt][:], func=AF.Exp),
                             reads=[t_tmp[tt]], writes=[t_PT[pt]])
                    elif bk == "const":
                        s.op("act", lambda e: e.activation(out=PT[pt][:], in_=PS[bank][:], func=AF.Exp,
                                                           bias=cst_sb[:, bi:bi + 1], scale=scale),
                             reads=[t_PS[bank], t_c], writes=[t_PT[pt]])
                    else:
                        s.op("act", lambda e: e.activation(out=PT[pt][:], in_=PS[bank][:], func=AF.Exp,
                                                           scale=scale),
                             reads=[t_PS[bank]], writes=[t_PT[pt]])
                    a = ACC[mp]
                    s.op("pe", lambda e: e.matmul(PS[a][0:65, :], lhsT=V_sb[:, kt, :], rhs=PT[pt][:],
                                                  start=first, stop=lastm),
                         reads=[t_V, t_PT[pt]], writes=[t_PS[a]])

                def finalize(qc):
                    for mp in range(nmaps):
                        a = ACC[mp]
                        s.op("act", lambda e, mp=mp, a=a: e.copy(out=o_sb[mp][:], in_=PS[a][0:65, :]),
                             reads=[t_PS[a]], writes=[t_o[mp]])
                        s.op("pe", lambda e, mp=mp: e.matmul(PS[PX][0:64, :], lhsT=sel32[:], rhs=o_sb[mp][:],
                                                             start=True, stop=True),
                             reads=[t_c, t_o[mp]], writes=[t_PS[PX]])
                        s.op("dve", lambda e: e.reciprocal(out=rden[:], in_=PS[PX][0:64, :]),
                             reads=[t_PS[PX]], writes=[t_rden])
                        if is_diff:
                            dst, tdst = on_sb[mp], t_on[mp]
                        else:
                            osl = octr[0] % 2
                            dst, tdst = ost[osl], t_ost[osl]
                        s.op("dve", lambda e, mp=mp, dst=dst: e.tensor_tensor(
                            out=dst[:], in0=o_sb[mp][0:64, :], in1=rden[:], op=ALU.mult),
                            reads=[t_o[mp], t_rden], writes=[tdst])
                    osl = octr[0] % 2
                    octr[0] += 1
                    if is_diff:
                        s.op("dve", lambda e: e.scalar_tensor_tensor(
                            out=od_sb[:], in0=on_sb[1][:], scalar=nlam, in1=on_sb[0][:],
                            op0=ALU.mult, op1=ALU.add), reads=[t_on[0], t_on[1], t_lam], writes=[t_od])
                        s.op("act", lambda e: e.activation(out=sq2[:], in_=od_sb[:], func=AF.Square),
                             reads=[t_od], writes=[t_sq2])
                        s.op("pe", lambda e: e.matmul(PS[PX][0:64, :], lhsT=ones64[:], rhs=sq2[:],
                                                      start=True, stop=True),
                             reads=[t_c, t_sq2], writes=[t_PS[PX]])
                        s.op("act", lambda e: e.activation(out=r2[:], in_=PS[PX][0:64, :], func=AF.Sqrt,
                                                           bias=eps2_sb[0:64, 0:1], scale=1.0 / (cmul * cmul)),
                             reads=[t_PS[PX], t_c], writes=[t_r2])
                        s.op("dve", lambda e: e.reciprocal(out=r2[:], in_=r2[:]), reads=[t_r2], writes=[t_r2])
                        s.op("dve", lambda e: e.scalar_tensor_tensor(
                            out=ost[osl][:], in0=od_sb[:], scalar=subln_sb[:, 0:1], in1=r2[:],
                            op0=ALU.mult, op1=ALU.mult), reads=[t_od, t_r2, t_c], writes=[t_ost[osl]])
                    s.dma("sp", lambda e: e.dma_start(out=oT[m, :, qc * QB:(qc + 1) * QB], in_=ost[osl][:]),
                          d_ost[osl], reads=[t_ost[osl]], writes=[t_out], accumulate=True)

                LOOK = 2
                load_q(0)
                if nqc > 1:
                    load_q(1)
                for i in range(min(LOOK, n)):
                    rec_qk(i)
                for i in range(n):
                    if i + LOOK < n:
                        rec_qk(i + LOOK)
                    rec_exp_pv(i)
                    if ents[i][7]:
                        qc = ents[i][0]
                        finalize(qc)
                        if qc + 2 < nqc:
                            load_q(qc + 2)
                gctr[0] += n

            sc64 = 64.0 ** -0.5

            def ents_na(qc):
                return [(0, kt, "tile", na_ids[(qc, kt)]) for kt in range(max(0, 4 * qc - 2), min(nkt, 4 * qc + 6))]

            def ents_diff(qc):
                out = []
                for mp in range(2):
                    for kt in range(nkt):
                        delta = kt * 128 - qc * QB
                        if delta <= -768:
                            out.append((mp, kt, "const", 0))
                        elif delta >= 1152:
                            out.append((mp, kt, "const", 1))
                        else:
                            out.append((mp, kt, "tile", (delta + 640) // 128))
                return out

            def ents_dil(qc):
                out = []
                for bi, (pi, delta) in enumerate(DIL_DELTAS):
                    k0 = qc * QB + delta
                    if 0 <= k0 < S:
                        out.append((0, k0 // 128, "tile", bi))
                return out

            def ents_mla(qc):
                return [(0, kt, None, 0) for kt in range(nkt)]

            mixers = build_attn.mixers if hasattr(build_attn, "mixers") else (0, 1, 2, 3)
            if 0 in mixers:
                run_mixer(0, 0, 0, 1, 0, 64, 1, sc64, ents_na, bias_na, n_na, False)
            if 1 in mixers:
                run_mixer(1, 0, 64, 1, 64, 64, 2, 32.0 ** -0.5, ents_diff, bias_diff, len(DIFF_DELTAS), True)
            if 2 in mixers:
                run_mixer(2, 2, 0, 2, 64, 64, 1, sc64, ents_dil, bias_dil, len(DIL_DELTAS), False)
            if 3 in mixers:
                run_mixer(3, 3, 0, 4, 0, 96, 1, 96.0 ** -0.5, ents_mla, None, 0, False)
        s.final_wait("sp")
        s.emit(st)
    return nc


def t5_bucket_np(rel):
    rel = np.asarray(rel, dtype=np.int64)
    nb, max_exact = 16, 8
    side = np.where(rel > 0, nb, 0)
    n = np.abs(rel)
    large = max_exact + (np.log(np.maximum(n, 1).astype(np.float32) / np.float32(max_exact))
                         / np.float32(math.log(1024 / max_exact)) * np.float32(nb - max_exact)).astype(np.int32)
    large = np.minimum(large, nb - 1)
    return side + np.where(n < max_exact, n, large)


def na_bias_tile(rpb_h, qc, kt, S):
    rows = S // 64
    wr = min(8, rows)
    k = kt * 128 + np.arange(128)[:, None]
    q = qc * QB + np.arange(QB)[None, :]
    kr, kc = k // 64, k % 64
    r, c = q // 64, q % 64
    r0 = np.clip(r - wr // 2, 0, rows - wr)
    c0 = np.clip(c - 8, 0, 64 - 16)
    valid = (kr >= r0) & (kr < r0 + wr) & (kc >= c0) & (kc < c0 + 16)
    ri = np.clip(kr - r + 7, 0, 14)
    cix = np.clip(kc - c + 15, 0, 30)
    return np.where(valid, rpb_h[ri, cix], np.float32(NEG)).astype(np.float32)


def diff_bias_tile(tab_h, delta):
    rel = delta + np.arange(128)[:, None] - np.arange(QB)[None, :]
    return tab_h[t5_bucket_np(rel)].astype(np.float32)


def dil_bias_tile(tab_h, pi, delta):
    window, dil = DIL_PATTERNS[pi]
    m = window // (2 * dil)
    rel = delta + np.arange(128)[:, None] - np.arange(QB)[None, :]
    valid = (rel % dil == 0) & (np.abs(rel) <= m * dil)
    return np.where(valid, tab_h[t5_bucket_np(rel)], np.float32(NEG)).astype(np.float32)


def rope_tables_np(S):
    half = 16
    inv_freq = (np.float32(10000.0) ** (-np.arange(half, dtype=np.float32) / np.float32(half))).astype(np.float32)
    ang = np.arange(S, dtype=np.float32)[:, None] * inv_freq[None, :]
    cos, sin = np.cos(ang).astype(np.float32).T, np.sin(ang).astype(np.float32).T
    cos2 = np.concatenate([cos, cos], 0)
    sin2 = np.concatenate([-sin, sin], 0)
    return np.ascontiguousarray(np.stack([cos2, sin2], 0))


def prep_attn_inputs(l, h, S, attn_norm, w_in, na_rpb, diff_lambda, diff_subln, mla_q_norm, w_uq,
                     mla_kv_norm, w_ukv, t5_table):
    W = w_in[l]
    hs = slice(h * 64, (h + 1) * 64)

    def grp(i):
        return W[:, i * 256:(i + 1) * 256][:, hs]

    qa, ka, va, qb, kb, vb, qc_, kc, vc = [grp(i) for i in range(9)]
    cq = W[:, 2304:2560]
    ckv = W[:, 2560:2688]
    kr = W[:, 2688:2720]
    z64 = np.zeros((D_MODEL, 64), np.float32)
    kr_sw = np.concatenate([kr[:, 16:32], kr[:, 0:16]], 1)
    cols = np.concatenate([qa, qb, ka, kb, qc_, kc, va, vb, vc, cq, ckv, z64, kr, z64, kr_sw], 1)
    assert cols.shape[1] == NW
    win_r = np.ascontiguousarray(cols.reshape(8, 128, NW).transpose(1, 0, 2)).reshape(128, 8 * NW)
    uq = w_uq[l][:, h * 96:(h + 1) * 96]
    rope_sw = np.concatenate([uq[:, 80:96], uq[:, 64:80]], 1)
    z = np.zeros((256, 64), np.float32)
    uq_cols = np.concatenate([uq, z, rope_sw], 1)
    wuq_r = np.ascontiguousarray(uq_cols.reshape(2, 128, 192).transpose(1, 0, 2)).reshape(128, 384)
    wukv_r = np.ascontiguousarray(w_ukv[l][:, h * 128:(h + 1) * 128])
    na_ids, na_gen = na_tile_ids(S)
    bias_na = np.stack([na_bias_tile(na_rpb[l, h], qc, kt, S) for (qc, kt) in na_gen], 0)
    td, tl = t5_table[:, h], t5_table[:, 4 + h]
    bias_diff = np.stack([diff_bias_tile(td, d) for d in DIFF_DELTAS], 0)
    bias_dil = np.stack([dil_bias_tile(tl, pi, d) for (pi, d) in DIL_DELTAS], 0)
    cst = np.ascontiguousarray(np.broadcast_to(np.array([td[15], td[31]], np.float32)[None, :], (128, 2)))
    return dict(
        gnA=np.ascontiguousarray(attn_norm[l].reshape(8, 128).T),
        win_r=win_r,
        gq=np.ascontiguousarray(mla_q_norm[l].reshape(2, 128).T),
        gkv=np.ascontiguousarray(mla_kv_norm[l].reshape(128, 1)),
        wuq_r=wuq_r, wukv_r=wukv_r,
        lam_in=np.ascontiguousarray(np.broadcast_to(diff_lambda[l].reshape(1, 128), (64, 128))),
        subln=np.ascontiguousarray(diff_subln[l].reshape(64, 1)),
        cst_diff=cst, bias_na=bias_na, bias_diff=bias_diff, bias_dil=bias_dil,
    )


SEQ = 16384
BATCH = 2
DEPTH = 2


def kernel(x, attn_norm, w_in, na_rpb, diff_lambda, diff_subln, mla_q_norm, w_uq, mla_kv_norm, w_ukv,
           t5_table, w_o, ffn_norm, w_gate, w_up, w_down, final_norm):
    f32 = lambda a: np.ascontiguousarray(np.asarray(a, dtype=np.float32))
    x = f32(x)
    (attn_norm, w_in, na_rpb, diff_lambda, diff_subln, mla_q_norm, w_uq, mla_kv_norm, w_ukv, t5_table, w_o,
     ffn_norm, w_gate, w_up, w_down, final_norm) = [f32(a) for a in (
         attn_norm, w_in, na_rpb, diff_lambda, diff_subln, mla_q_norm, w_uq, mla_kv_norm, w_ukv, t5_table, w_o,
         ffn_norm, w_gate, w_up, w_down, final_norm)]
    B, S, D = x.shape
    TB = S // 4
    xT = [np.ascontiguousarray(x[b].T) for b in range(B)]
    rope_tab = rope_tables_np(S)
    cores = list(range(NCORES))
    for l in range(DEPTH):
        nc = build_attn(S, l)
        in_maps = []
        for c in cores:
            b, h = c // 4, c % 4
            m = prep_attn_inputs(l, h, S, attn_norm, w_in, na_rpb, diff_lambda, diff_subln, mla_q_norm, w_uq,
                                 mla_kv_norm, w_ukv, t5_table)
            m["xT"] = xT[b]
            m["rope_tab"] = rope_tab
            in_maps.append(m)
        res = run_bass_kernel_spmd(nc, in_maps, core_ids=cores)
        mixT = [np.empty((D, S), np.float32) for _ in range(B)]
        for c in cores:
            b, h = c // 4, c % 4
            o = res.results[c]["oT"]
            for m in range(4):
                mixT[b][m * 256 + h * 64:m * 256 + (h + 1) * 64, :] = o[m]
        del res, in_maps
        last = (l == DEPTH - 1)
        nc = build_ffn(TB, last)
        wts = prep_ffn_weights(w_o[l], ffn_norm[l], w_gate[l], w_up[l], w_down[l], final_norm)
        in_maps = []
        for c in cores:
            b, j = c // 4, c % 4
            m = dict(wts)
            m["xT"] = np.ascontiguousarray(xT[b][:, j * TB:(j + 1) * TB])
            m["mixT"] = np.ascontiguousarray(mixT[b][:, j * TB:(j + 1) * TB])
            in_maps.append(m)
        res = run_bass_kernel_spmd(nc, in_maps, core_ids=cores)
        xT = [np.empty((D, S), np.float32) for _ in range(B)]
        for c in cores:
            b, j = c // 4, c % 4
            xT[b][:, j * TB:(j + 1) * TB] = res.results[c]["yT"]
        del res, in_maps
    return np.ascontiguousarray(np.stack([xT[b].T for b in range(B)], 0))
```
